# Optimizing a Trainium2 kernel written in Bass

```python
import jax, jax.numpy as jnp
from jax import lax
import numpy as np

D_MODEL = 1024
BATCH = 4
SEQ = 4096
DEPTH = 4

D_MIX = D_MODEL
D_POOL = D_MIX // 2
D_HGRN = D_MIX - D_POOL
POOL_WINDOWS = (2, 4, 8, 16)
N_POOL_GROUPS = len(POOL_WINDOWS)
POOL_GROUP_DIM = D_POOL // N_POOL_GROUPS
HGRN_HEADS = 4
HGRN_HEAD_DIM = D_HGRN // HGRN_HEADS
CHUNK = 64
D_IN = D_POOL + 4 * D_HGRN
D_FF = ((8 * D_MODEL // 3 + 127) // 128) * 128
ALPHA = (2.0 * DEPTH) ** 0.25
BETA = (8.0 * DEPTH) ** -0.25
LN_EPS = 1e-5
RMS_EPS = 1e-6

kernel_name = "macaron_pool_hgrn2_deepnorm_hybrid"


def layer_norm(x, g, b):
    x32 = x.astype(jnp.float32)
    mu = jnp.mean(x32, axis=-1, keepdims=True)
    var = jnp.mean(jnp.square(x32 - mu), axis=-1, keepdims=True)
    y = (x32 - mu) * lax.rsqrt(var + LN_EPS) * g.astype(jnp.float32) + b.astype(jnp.float32)
    return y.astype(x.dtype)


def swiglu(x, w_gate, w_up, w_down):
    return (jax.nn.silu(x @ w_gate) * (x @ w_up)) @ w_down


def multiscale_pool(u, pool_w, pool_scale):
    B, S, _ = u.shape
    u32 = u.astype(jnp.float32).reshape(B, S, N_POOL_GROUPS, POOL_GROUP_DIM)
    c = jnp.cumsum(u32, axis=1)
    pos = jnp.arange(S)
    means = []
    for g, w in enumerate(POOL_WINDOWS):
        cg = c[:, :, g]
        lag = jnp.pad(cg, ((0, 0), (w, 0), (0, 0)))[:, :S]
        cnt = jnp.minimum(pos + 1, w).astype(jnp.float32)[None, :, None]
        means.append((cg - lag) / cnt)
    pooled = jnp.stack(means, axis=2) - u32
    y = jnp.einsum('bsgc,gcd->bsgd', pooled.astype(u.dtype), pool_w)
    return y.reshape(B, S, D_POOL) * pool_scale


def hgrn2_recurrence(q, f_raw, v, lb):
    B, S, _ = q.shape
    n = S // CHUNK
    log_f = jnp.logaddexp(jnp.log(lb), jnp.log1p(-lb) + jax.nn.log_sigmoid(f_raw))
    k = (1.0 - lb) * jax.nn.sigmoid(-f_raw)
    q = q * (HGRN_HEAD_DIM ** -0.5)

    def heads(t):
        return t.reshape(B, n, CHUNK, HGRN_HEADS, HGRN_HEAD_DIM).transpose(1, 0, 3, 2, 4)

    causal = jnp.tril(jnp.ones((CHUNK, CHUNK), dtype=bool))[:, :, None]

    def step(state, inp):
        q_c, k_c, v_c, g_c = inp
        b = jnp.cumsum(g_c, axis=2)
        o_inter = jnp.einsum('bhtd,bhdv->bhtv', q_c * jnp.exp(b), state)
        rel = b[:, :, :, None, :] - b[:, :, None, :, :]
        decay = jnp.exp(jnp.where(causal, rel, -jnp.inf))
        scores = jnp.einsum('bhtd,bhsd,bhtsd->bhts', q_c, k_c, decay)
        o_intra = jnp.einsum('bhts,bhsv->bhtv', scores, v_c)
        b_last = b[:, :, -1:, :]
        k_dec = k_c * jnp.exp(b_last - b)
        new_state = jnp.exp(b_last[:, :, 0, :])[..., None] * state + jnp.einsum('bhsd,bhsv->bhdv', k_dec, v_c)
        return new_state, o_inter + o_intra

    s0 = jnp.zeros((B, HGRN_HEADS, HGRN_HEAD_DIM, HGRN_HEAD_DIM), jnp.float32)
    _, o = lax.scan(step, s0, (heads(q), heads(k), heads(v), heads(log_f)))
    return o.transpose(1, 0, 3, 2, 4).reshape(B, S, D_HGRN)


def head_rms_norm(o, g):
    B, S, _ = o.shape
    o = o.reshape(B, S, HGRN_HEADS, HGRN_HEAD_DIM)
    o = o * lax.rsqrt(jnp.mean(jnp.square(o), axis=-1, keepdims=True) + RMS_EPS) * g.astype(jnp.float32)
    return o.reshape(B, S, D_HGRN)


def setup_inputs(seed: int = 0) -> dict:
    key = jax.random.key(seed)
    ks = jax.random.split(key, 16)
    nrm = jax.random.normal
    f32 = jnp.float32
    x = nrm(ks[0], (BATCH, SEQ, D_MODEL), f32)
    w_in = nrm(ks[1], (DEPTH, D_MODEL, D_IN), f32) * D_MODEL ** -0.5
    pool_w = nrm(ks[2], (DEPTH, N_POOL_GROUPS, POOL_GROUP_DIM, POOL_GROUP_DIM), f32) * POOL_GROUP_DIM ** -0.5
    pool_scale = 1.0 + 0.02 * nrm(ks[3], (DEPTH, D_POOL), f32)
    lb_param = 0.5 * nrm(ks[4], (DEPTH, D_HGRN), f32)
    hgrn_norm_g = 1.0 + 0.02 * nrm(ks[5], (DEPTH, HGRN_HEAD_DIM), f32)
    w_out = nrm(ks[6], (DEPTH, D_MIX, D_MODEL), f32) * (D_MIX ** -0.5 * BETA)
    ffn1_gate = nrm(ks[7], (DEPTH, D_MODEL, D_FF), f32) * D_MODEL ** -0.5
    ffn1_up = nrm(ks[8], (DEPTH, D_MODEL, D_FF), f32) * D_MODEL ** -0.5
    ffn1_down = nrm(ks[9], (DEPTH, D_FF, D_MODEL), f32) * (D_FF ** -0.5 * BETA)
    ffn2_gate = nrm(ks[10], (DEPTH, D_MODEL, D_FF), f32) * D_MODEL ** -0.5
    ffn2_up = nrm(ks[11], (DEPTH, D_MODEL, D_FF), f32) * D_MODEL ** -0.5
    ffn2_down = nrm(ks[12], (DEPTH, D_FF, D_MODEL), f32) * (D_FF ** -0.5 * BETA)
    ln_g = 1.0 + 0.02 * nrm(ks[13], (DEPTH, 3, D_MODEL), f32)
    ln_b = 0.02 * nrm(ks[14], (DEPTH, 3, D_MODEL), f32)
    return {"x": x, "w_in": w_in, "pool_w": pool_w, "pool_scale": pool_scale, "lb_param": lb_param,
            "hgrn_norm_g": hgrn_norm_g, "w_out": w_out, "ffn1_gate": ffn1_gate, "ffn1_up": ffn1_up,
            "ffn1_down": ffn1_down, "ffn2_gate": ffn2_gate, "ffn2_up": ffn2_up, "ffn2_down": ffn2_down,
            "ln_g": ln_g, "ln_b": ln_b}


def reference(x, w_in, pool_w, pool_scale, lb_param, hgrn_norm_g, w_out, ffn1_gate, ffn1_up, ffn1_down,
              ffn2_gate, ffn2_up, ffn2_down, ln_g, ln_b):
    f32 = jnp.float32
    lb_all = jnp.cumsum(jax.nn.softmax(lb_param.astype(f32), axis=0), axis=0)
    lb_all = lb_all - lb_all[0:1]
    for l in range(DEPTH):
        x = layer_norm(ALPHA * x + 0.5 * swiglu(x, ffn1_gate[l], ffn1_up[l], ffn1_down[l]), ln_g[l, 0], ln_b[l, 0])
        h = x @ w_in[l]
        u = h[..., :D_POOL]
        q, f_raw, v, g = jnp.split(h[..., D_POOL:], 4, axis=-1)
        y_pool = multiscale_pool(u, pool_w[l], pool_scale[l])
        o = hgrn2_recurrence(q.astype(f32), f_raw.astype(f32), v.astype(f32), lb_all[l])
        y_hgrn = head_rms_norm(o, hgrn_norm_g[l]) * jax.nn.silu(g.astype(f32))
        mix = jnp.concatenate([y_pool.astype(x.dtype), y_hgrn.astype(x.dtype)], axis=-1) @ w_out[l]
        x = layer_norm(ALPHA * x + mix, ln_g[l, 1], ln_b[l, 1])
        x = layer_norm(ALPHA * x + 0.5 * swiglu(x, ffn2_gate[l], ffn2_up[l], ffn2_down[l]), ln_g[l, 2], ln_b[l, 2])
    return x
```

```python
import contextlib
import numpy as np
import concourse.bass as bass
import concourse.mybir as mybir
from concourse.bass_utils import run_bass_kernel_spmd

F32 = mybir.dt.float32
BF16 = mybir.dt.bfloat16
AF = mybir.ActivationFunctionType
ALU = mybir.AluOpType

D = 1024
DFF = 2816
NFC = DFF // 128
NKC = D // 128
DEPTH = 4
SEQ = 4096
TS = 1024
NTT = TS // 128
NBLK = TS // 512
CH = 64
NCH = TS // CH
ALPHA = (2.0 * DEPTH) ** 0.25
LN_EPS = 1e-5
RMS_EPS = 1e-6
POOL_W = (2, 4, 8, 16)
PH = [(0, 6), (6, 12), (12, 17), (17, 22)]


class Ev:
    __slots__ = ("sem", "val", "eng")

    def __init__(self, sem, val, eng):
        self.sem, self.val, self.eng = sem, val, eng


class Buf:
    __slots__ = ("w", "r")

    def __init__(self):
        self.w = None
        self.r = {}


class Sched:
    def __init__(self, nc, stack):
        self.nc = nc
        self.stack = stack
        self.engs = {"pe": nc.tensor, "act": nc.scalar, "dve": nc.vector, "pool": nc.gpsimd, "sp": nc.sync}
        self.sem = {}
        self.cnt = {}
        self.seen = {e: {} for e in self.engs}
        self.bufs = {}
        self.dsem = {}
        self.nsem = 0
        self.last_dma = []
        self.new_epoch()

    def _newsem(self, name):
        self.nsem += 1
        return self.stack.enter_context(self.nc.semaphore(f"{name}_{self.nsem}"))

    def new_epoch(self):
        for e in ("pe", "act", "dve", "pool"):
            self.sem[e] = self._newsem("s" + e)
            self.cnt[e] = 0

    def _buf(self, k):
        b = self.bufs.get(k)
        if b is None:
            b = self.bufs[k] = Buf()
        return b

    def _wait(self, eng, ev):
        if ev is None:
            return
        if eng == "pe" and ev.eng == "pe":
            return
        key = id(ev.sem)
        if self.seen[eng].get(key, 0) >= ev.val:
            return
        self.engs[eng].wait_ge(ev.sem, ev.val)
        self.seen[eng][key] = ev.val

    def _deps(self, eng, reads, writes):
        for k in reads:
            self._wait(eng, self._buf(k).w)
        for k in writes:
            b = self._buf(k)
            self._wait(eng, b.w)
            for ev in b.r.values():
                self._wait(eng, ev)

    def _commit(self, ev, reads, writes):
        for k in reads:
            self._buf(k).r[ev.eng] = ev
        for k in writes:
            b = self._buf(k)
            b.w = ev
            b.r = {}

    def op(self, eng, fns, reads=(), writes=()):
        if callable(fns):
            fns = [fns]
        self._deps(eng, reads, writes)
        inst = None
        for f in fns:
            inst = f()
        self.cnt[eng] += 1
        inst.then_inc(self.sem[eng], 1)
        ev = Ev(self.sem[eng], self.cnt[eng], eng)
        self._commit(ev, reads, writes)
        return ev

    def dma(self, q, out, in_, reads=(), writes=(), key=None, **kw):
        self._deps(q, reads, writes)
        if key not in self.dsem:
            self.dsem[key] = [self._newsem("d"), 0]
        st = self.dsem[key]
        st[1] += 16
        self.engs[q].dma_start(out=out, in_=in_, **kw).then_inc(st[0], 16)
        ev = Ev(st[0], st[1], "dma:" + str(key))
        self._commit(ev, reads, writes)
        return ev

    def wait_all(self, eng, keys):
        for k in keys:
            b = self._buf(k)
            self._wait(eng, b.w)
            for ev in b.r.values():
                self._wait(eng, ev)


class _Stop(Exception):
    pass


class Prog:
    def __init__(self, nseg, nlayers=DEPTH, stop_after=None, skip_ffn=False):
        self.skip_ffn = skip_ffn
        self.nseg = nseg
        self.nlayers = nlayers
        self.stop_after = stop_after
        self.nc = bass.Bass("TRN2", target_bir_lowering=False)
        self.stack = contextlib.ExitStack()

    def sb(self, name, shape, dt):
        return self.stack.enter_context(self.nc.sbuf_tensor(name, shape, dt))

    def din(self, name, shape, dt=F32):
        return self.nc.dram_tensor(name, shape, dt, kind="ExternalInput").ap()

    def build(self):
        nc = self.nc
        L = DEPTH
        S = self.nseg * TS
        with self.stack:
            self.s = Sched(nc, self.stack)
            self.x = self.din("x", [S, D])
            self.w_in = self.din("w_in", [L, D, 2560])
            self.pool_w = self.din("pool_w", [L, 4, 128, 128])
            self.pool_scale = self.din("pool_scale", [L, 512])
            self.lb_param = self.din("lb_param", [L, 512])
            self.hgn = self.din("hgrn_norm_g", [L, 128])
            self.w_out = self.din("w_out", [L, D, D])
            self.fg = [self.din("ffn1_gate", [L, D, DFF]), self.din("ffn2_gate", [L, D, DFF])]
            self.fu = [self.din("ffn1_up", [L, D, DFF]), self.din("ffn2_up", [L, D, DFF])]
            self.fd = [self.din("ffn1_down", [L, DFF, D]), self.din("ffn2_down", [L, DFF, D])]
            self.ln_g = self.din("ln_g", [L, 3, D])
            self.ln_b = self.din("ln_b", [L, 3, D])
            self.c_ident = self.din("c_ident", [128, 128])
            self.c_mask = self.din("c_mask", [128, 128])
            self.c_rc = self.din("c_rc", [128, 64])
            self.c_rowm = self.din("c_rowm", [128, 2])
            self.out = nc.dram_tensor("out", [S, D], F32, kind="ExternalOutput").ap()

            self.X = self.sb("X", [128, NTT, D], F32)
            self.XT = self.sb("XT", [128, NKC, TS], BF16)
            self.ACTT = self.sb("ACTT", [128, 6, TS], BF16)
            self.NU = 4
            self.uslot = [self.sb(f"us{i}", [128, NKC, 256], BF16) for i in range(self.NU)]
            self.ND = 8
            self.dslot = [self.sb(f"ds{i}", [128, D], BF16) for i in range(self.ND)]
            self.ui = 0
            self.di = 0
            self.Gt = self.sb("Gt", [128, D], F32)
            self.Bt = self.sb("Bt", [128, D], F32)
            self.sg = [self.sb(f"sg{i}", [128, 512], F32) for i in range(2)]
            self.sgi = 0
            self.ident = self.sb("ident", [128, 128], F32)
            self.stats = self.sb("stats", [128, NTT, 2, 6], F32)
            self.mv = self.sb("mv", [128, NTT, 2], F32)
            self.lnt = self.sb("lnt", [128, NTT], F32)
            self.rstd = self.sb("rstd", [128, NTT], F32)
            self.nmr = self.sb("nmr", [128, NTT], F32)
            self.NPS = 7
            self.ps = [self.stack.enter_context(nc.psum_tensor(f"ps{i}", [128, 512], F32)) for i in range(self.NPS)]
            self.psb = self.stack.enter_context(nc.psum_tensor("psb", [128, 1024], BF16))
            self.psi = 0
            self.alloc_mixer()

            self.emit()
        return nc

    def chk(self, name):
        if self.stop_after == name:
            raise _Stop()

    def psum(self):
        i = self.psi
        self.psi = (self.psi + 1) % self.NPS
        return i

    def load_u(self, src, ncols):
        i = self.ui
        self.ui = (self.ui + 1) % self.NU
        t = self.uslot[i]
        self.s.dma("pool", t[:, :, 0:ncols], src.rearrange("(kc p) n -> p kc n", p=128),
                   writes=[("us", i)], key=("us", i))
        return i

    def load_d(self, src):
        i = self.di
        self.di = (self.di + 1) % self.ND
        self.s.dma("pool", self.dslot[i][:], src, writes=[("ds", i)], key=("ds", i))
        return i

    def emit(self):
        nc, s = self.nc, self.s
        s.dma("sp", self.ident[:], self.c_ident, writes=["ident"], key="ident")
        self.setup()
        for seg in range(self.nseg):
            self.seg = seg
            t0 = seg * TS
            for tt in range(NTT):
                s.dma("sp", self.X[:, tt, :], self.x[t0 + tt * 128: t0 + (tt + 1) * 128, :],
                      writes=[("X", tt, 0), ("X", tt, 1)], key=("xin", tt))
            for tt in range(NTT):
                self.transpose_tile(tt)
            try:
                for l in range(self.nlayers):
                    self.chk("setup")
                    if l % 2 == 0:
                        s.new_epoch()
                    if not self.skip_ffn:
                        self.ffn(l, 0)
                        self.chk("ffn1")
                        self.layernorm(l, 0)
                        self.chk("ln0")
                    self.mixer(l)
                    self.chk("mix")
                    self.layernorm(l, 1)
                    self.chk("ln1")
                    self.ffn(l, 1)
                    self.layernorm(l, 2)
            except _Stop:
                pass
            for tt in range(NTT):
                s.dma("sp", self.out[t0 + tt * 128: t0 + (tt + 1) * 128, :], self.X[:, tt, :],
                      reads=[("X", tt, 0), ("X", tt, 1)], key=("xout", tt))
        for tt in range(NTT):
            st = s.dsem[("xout", tt)]
            nc.sync.wait_ge(st[0], st[1])

    def transpose_tile(self, tt):
        nc, s = self.nc, self.s
        for hh in range(2):
            b = self.psum()
            pt = self.ps[b]
            fns = []
            for j in range(4):
                kc = hh * 4 + j
                fns.append(lambda j=j, kc=kc: nc.tensor.transpose(
                    pt[:, j * 128:(j + 1) * 128], self.X[:, tt, kc * 128:(kc + 1) * 128], self.ident[:]))
            s.op("pe", fns, reads=[("X", tt, hh), "ident"], writes=[("ps", b)])
            s.op("act", lambda: nc.scalar.activation(
                out=self.XT[:, hh * 4:(hh + 1) * 4, tt * 128:(tt + 1) * 128],
                in_=pt[:].rearrange("p (a b) -> p a b", b=128), func=AF.Copy),
                reads=[("ps", b)], writes=[("XT", tt, hh)])

    def xt_keys(self, tts):
        return [("XT", tt, hh) for tt in tts for hh in range(2)]

    def ffn(self, l, which):
        nc, s = self.nc, self.s
        Wg, Wu, Wd = self.fg[which][l], self.fu[which][l], self.fd[which][l]
        for pi, (c0, c1) in enumerate(PH):
            dsl = None
            units = list(range(c0, c1, 2))
            for ui_, cp in enumerate(units):
                nw = min(2, c1 - cp)
                ig = self.load_u(Wg[:, cp * 128:(cp + nw) * 128], nw * 128)
                iu = self.load_u(Wu[:, cp * 128:(cp + nw) * 128], nw * 128)
                if ui_ == 0:
                    dsl = [self.load_d(Wd[c * 128:(c + 1) * 128, :]) for c in range(c0, c1)]
                for ci in range(nw):
                    c = cp + ci
                    for nb in range(NBLK):
                        bg, bu = self.psum(), self.psum()
                        tsl = slice(nb * 512, (nb + 1) * 512)
                        xk = self.xt_keys(range(nb * 4, nb * 4 + 4))
                        fns = [lambda kc=kc: nc.tensor.matmul(
                            self.ps[bg][:], lhsT=self.uslot[ig][:, kc, ci * 128:(ci + 1) * 128],
                            rhs=self.XT[:, kc, tsl], start=(kc == 0), stop=(kc == NKC - 1)) for kc in range(NKC)]
                        s.op("pe", fns, reads=[("us", ig)] + xk, writes=[("ps", bg)])
                        fns = [lambda kc=kc: nc.tensor.matmul(
                            self.ps[bu][:], lhsT=self.uslot[iu][:, kc, ci * 128:(ci + 1) * 128],
                            rhs=self.XT[:, kc, tsl], start=(kc == 0), stop=(kc == NKC - 1)) for kc in range(NKC)]
                        s.op("pe", fns, reads=[("us", iu)] + xk, writes=[("ps", bu)])
                        si = self.sgi
                        self.sgi ^= 1
                        s.op("act", lambda: nc.scalar.activation(out=self.sg[si][:], in_=self.ps[bg][:], func=AF.Silu),
                             reads=[("ps", bg)], writes=[("sg", si)])
                        s.op("dve", lambda: nc.vector.scalar_tensor_tensor(
                            out=self.ACTT[:, c - c0, tsl], in0=self.sg[si][:], scalar=0.5, in1=self.ps[bu][:],
                            op0=ALU.mult, op1=ALU.mult),
                            reads=[("sg", si), ("ps", bu)], writes=[("ACTT", c - c0, nb)])
            ncn = c1 - c0
            for tt in range(NTT):
                for hh in range(2):
                    b = self.psum()
                    fns = [lambda j=j: nc.tensor.matmul(
                        self.ps[b][:], lhsT=self.ACTT[:, j, tt * 128:(tt + 1) * 128],
                        rhs=self.dslot[dsl[j]][:, hh * 512:(hh + 1) * 512], start=(j == 0), stop=(j == ncn - 1))
                        for j in range(ncn)]
                    s.op("pe", fns, reads=[("ACTT", j, tt // 4) for j in range(ncn)] + [("ds", dsl[j]) for j in range(ncn)],
                         writes=[("ps", b)])
                    xs = self.X[:, tt, hh * 512:(hh + 1) * 512]
                    if pi == 0:
                        s.op("dve", lambda: nc.vector.scalar_tensor_tensor(
                            out=xs, in0=xs, scalar=ALPHA, in1=self.ps[b][:], op0=ALU.mult, op1=ALU.add),
                            reads=[("ps", b)], writes=[("X", tt, hh)])
                    else:
                        s.op("dve", lambda: nc.vector.tensor_tensor(out=xs, in0=xs, in1=self.ps[b][:], op=ALU.add),
                             reads=[("ps", b)], writes=[("X", tt, hh)])

    def layernorm(self, l, j):
        nc, s = self.nc, self.s
        s.dma("sp", self.Gt[:], self.ln_g[l, j:j + 1, :].partition_broadcast(128), writes=["Gt"], key="Gt")
        s.dma("sp", self.Bt[:], self.ln_b[l, j:j + 1, :].partition_broadcast(128), writes=["Bt"], key="Bt")
        for tt in range(NTT):
            for hh in range(2):
                s.op("dve", lambda: nc.vector.bn_stats(out=self.stats[:, tt, hh, :], in_=self.X[:, tt, hh * 512:(hh + 1) * 512]),
                     reads=[("X", tt, hh)], writes=[("stats", tt, hh)])
            s.op("dve", lambda: nc.vector.bn_aggr(out=self.mv[:, tt, :], in_=self.stats[:, tt, :, :]),
                 reads=[("stats", tt, 0), ("stats", tt, 1)], writes=[("mv", tt)])
        mvk = [("mv", tt) for tt in range(NTT)]
        s.op("act", lambda: nc.scalar.activation(out=self.lnt[:], in_=self.mv[:, :, 1], func=AF.Ln, bias=LN_EPS, scale=1.0),
             reads=mvk, writes=["lnt"])
        s.op("act", lambda: nc.scalar.activation(out=self.rstd[:], in_=self.lnt[:], func=AF.Exp, scale=-0.5),
             reads=["lnt"], writes=["rstd"])
        s.op("dve", lambda: nc.vector.scalar_tensor_tensor(out=self.nmr[:], in0=self.mv[:, :, 0], scalar=-1.0, in1=self.rstd[:],
                                                          op0=ALU.mult, op1=ALU.mult),
             reads=mvk + ["rstd"], writes=["nmr"])
        for tt in range(NTT):
            xk = [("X", tt, 0), ("X", tt, 1)]
            s.op("act", lambda: nc.scalar.activation(out=self.X[:, tt, :], in_=self.X[:, tt, :], func=AF.Identity,
                                                     bias=self.nmr[:, tt:tt + 1], scale=self.rstd[:, tt:tt + 1]),
                 reads=["nmr", "rstd"], writes=xk)
            s.op("dve", lambda: nc.vector.tensor_tensor(out=self.X[:, tt, :], in0=self.X[:, tt, :], in1=self.Gt[:], op=ALU.mult),
                 reads=["Gt"], writes=xk)
            s.op("dve", lambda: nc.vector.tensor_tensor(out=self.X[:, tt, :], in0=self.X[:, tt, :], in1=self.Bt[:], op=ALU.add),
                 reads=["Bt"], writes=xk)
            self.transpose_tile(tt)

    def alloc_mixer(self):
        sb = self.sb
        L = DEPTH
        self.tb = [sb(f"tb{i}", [128, 16 + TS], F32) for i in range(4)]
        self.QT = sb("QT", [128, TS], BF16)
        self.KT = sb("KT", [128, TS], BF16)
        self.vtok = sb("vtok", [128, NTT, 256], BF16)
        self.ktokm = [sb(f"ktokm{i}", [128, NTT, 128], BF16) for i in range(2)]
        self.rowm = sb("rowm", [128, 2], F32)
        self.data0 = sb("data0", [128, 128, NCH + 1], F32)
        self.data1 = sb("data1", [128, 128, NCH + 1], F32)
        self.Sall = sb("Sall", [128, 128, NCH + 1], F32)
        self.St = sb("St", [128, NCH, 128], BF16)
        self.AT = sb("AT", [128, 4, 128], BF16)
        self.osq = self.sg[0]
        self.rsd = sb("rsd", [128, 512], F32)
        self.ytmp = sb("ytmp", [128, 512], F32)
        self.sgate = self.sg[1]
        self.pooledT = sb("pooledT", [128, TS], BF16)
        self.mixT = sb("mixT", [128, NKC, TS], BF16)
        self.poolw = sb("poolw", [128, 4, 128], BF16)
        self.e1 = sb("e1", [128, NCH], F32)
        self.e2 = sb("e2", [128, NCH], F32)
        self.Eb = sb("Eb", [128, NCH], F32)
        self.tmp16 = sb("tmp16", [128, NCH], F32)
        self.Sst = sb("Sst", [128, L, 4, 128], F32)
        self.utail = sb("utail", [128, L, 4, 16], F32)
        self.LB = sb("LB", [128, 4, L], F32)
        self.lbx = sb("lbx", [128, 4, L], F32)
        self.lbs = sb("lbs", [128, 4], F32)
        self.gn = sb("gn", [128, L], F32)
        self.pscale = sb("pscale", [128, 4, L], F32)
        self.mask = sb("mask", [128, 128], F32)
        self.rc = sb("rc", [128, 64], F32)
        self.ones = sb("ones", [128, 128], F32)
        self.identb = sb("identb", [128, 128], BF16)
        self.rmask = sb("rmask", [128, TS], F32)

    def setup(self):
        nc, s = self.nc, self.s
        s.dma("sp", self.mask[:], self.c_mask, writes=["mask"], key="mask")
        s.dma("sp", self.rc[:], self.c_rc, writes=["rc"], key="rc")
        s.dma("sp", self.rowm[:], self.c_rowm, writes=["rowm"], key="rowm")
        for h in range(4):
            s.dma("sp", self.lbx[:, h, :], self.lb_param[:, h * 128:(h + 1) * 128].rearrange("l d -> d l"), writes=["lbx"],
                  key=("lbx", h), allow_slow_non_contiguous=True)
        s.dma("sp", self.gn[:], self.hgn.rearrange("l d -> d l"), writes=["gn"], key="gn", allow_slow_non_contiguous=True)
        for g in range(4):
            s.dma("sp", self.pscale[:, g, :], self.pool_scale[:, g * 128:(g + 1) * 128].rearrange("l d -> d l"), writes=["pscale"],
                  key=("pscale", g), allow_slow_non_contiguous=True)
        s.op("dve", lambda: nc.vector.memset(self.ones[:], 1.0), writes=["ones"])
        s.op("dve", lambda: nc.vector.tensor_copy(out=self.identb[:], in_=self.ident[:]), reads=["ident"], writes=["identb"])
        s.op("dve", lambda: nc.vector.memset(self.rmask[:], 1.0), writes=["rmask"])
        s.op("dve", lambda: nc.vector.memset(self.rmask[:].rearrange("p (c j) -> p c j", j=CH)[:, :, 0:1], 0.0), writes=["rmask"])
        s.op("dve", lambda: nc.vector.memset(self.data0[:], 0.0), writes=["data0"])
        s.op("dve", lambda: nc.vector.memset(self.Sst[:], 0.0), writes=["Sst"])
        s.op("dve", lambda: nc.vector.memset(self.utail[:], 0.0), writes=["utail"])
        s.op("dve", lambda: nc.vector.tensor_reduce(out=self.lbs[:], in_=self.lbx[:], axis=mybir.AxisListType.X, op=ALU.max),
             reads=["lbx"], writes=["lbs"])
        s.op("dve", lambda: nc.vector.tensor_tensor(out=self.lbx[:], in0=self.lbx[:],
                                                    in1=self.lbs[:].unsqueeze(2).to_broadcast([128, 4, DEPTH]), op=ALU.subtract),
             reads=["lbs"], writes=["lbx"])
        s.op("act", lambda: nc.scalar.activation(out=self.lbx[:], in_=self.lbx[:], func=AF.Exp), writes=["lbx"])
        s.op("dve", lambda: nc.vector.tensor_reduce(out=self.lbs[:], in_=self.lbx[:], axis=mybir.AxisListType.X, op=ALU.add),
             reads=["lbx"], writes=["lbs"])
        s.op("dve", lambda: nc.vector.reciprocal(out=self.lbs[:], in_=self.lbs[:]), writes=["lbs"])
        s.op("dve", lambda: nc.vector.memset(self.LB[:], 0.0), writes=["LB"])
        for l in range(1, DEPTH):
            s.op("dve", lambda: nc.vector.tensor_tensor(out=self.LB[:, :, l], in0=self.LB[:, :, l - 1], in1=self.lbx[:, :, l], op=ALU.add),
                 reads=["lbx"], writes=["LB"])
        s.op("dve", lambda: nc.vector.tensor_tensor(out=self.LB[:], in0=self.LB[:],
                                                    in1=self.lbs[:].unsqueeze(2).to_broadcast([128, 4, DEPTH]), op=ALU.mult),
             reads=["lbs"], writes=["LB"])

    def proj_fm(self, slot, cs, nb, extra_reads=()):
        nc, s = self.nc, self.s
        b = self.psum()
        tsl = slice(nb * 512, (nb + 1) * 512)
        fns = [lambda kc=kc: nc.tensor.matmul(self.ps[b][:], lhsT=self.uslot[slot][:, kc, cs], rhs=self.XT[:, kc, tsl],
                                              start=(kc == 0), stop=(kc == NKC - 1)) for kc in range(NKC)]
        s.op("pe", fns, reads=[("us", slot)] + self.xt_keys(range(nb * 4, nb * 4 + 4)) + list(extra_reads), writes=[("ps", b)])
        return b

    def body(self, t):
        return t[:, 16:16 + TS]

    def pool_group(self, l, g, slot):
        nc, s = self.nc, self.s
        gi = g % 2
        cs = slice(gi * 128, (gi + 1) * 128)
        T0, T1, T2, T3 = self.tb
        for nb in range(NBLK):
            b = self.proj_fm(slot, cs, nb)
            s.op("act", lambda: nc.scalar.activation(out=T0[:, 16 + nb * 512:16 + (nb + 1) * 512], in_=self.ps[b][:], func=AF.Copy),
                 reads=[("ps", b)], writes=["t0"])
        s.op("act", lambda: nc.scalar.activation(out=T0[:, 0:16], in_=self.utail[:, l, g, :], func=AF.Copy),
             reads=["utail"], writes=["t0"])
        s.op("act", lambda: nc.scalar.activation(out=self.utail[:, l, g, :], in_=T0[:, TS:TS + 16], func=AF.Copy),
             reads=["t0"], writes=["utail"])
        cur, curk, lo = T0, "t0", 0
        for k in range(g + 1):
            sh = 1 << k
            lo2 = lo + sh
            dst, dstk = (T1, "t1") if cur is not T1 else (T2, "t2")
            s.op("dve", lambda: nc.vector.tensor_tensor(out=dst[:, lo2:16 + TS], in0=cur[:, lo2:16 + TS],
                                                        in1=cur[:, lo2 - sh:16 + TS - sh], op=ALU.add),
                 reads=[curk], writes=[dstk])
            cur, curk, lo = dst, dstk, lo2
        w = POOL_W[g]
        s.op("dve", lambda: nc.vector.scalar_tensor_tensor(out=self.pooledT[:], in0=self.body(cur), scalar=1.0 / w,
                                                          in1=self.body(T0), op0=ALU.mult, op1=ALU.subtract),
             reads=[curk, "t0"], writes=["pooledT"])
        if self.seg == 0:
            s.op("dve", lambda: nc.vector.tensor_tensor(out=T3[:, 0:16], in0=cur[:, 16:32], in1=self.rc[:, g * 16:(g + 1) * 16], op=ALU.mult),
                 reads=[curk, "rc"], writes=["t3"])
            s.op("dve", lambda: nc.vector.tensor_tensor(out=self.pooledT[:, 0:16], in0=T3[:, 0:16], in1=T0[:, 16:32], op=ALU.subtract),
                 reads=["t3", "t0"], writes=["pooledT"])
        for nb in range(NBLK):
            b = self.psum()
            tsl = slice(nb * 512, (nb + 1) * 512)
            s.op("pe", lambda: nc.tensor.matmul(self.ps[b][:], lhsT=self.poolw[:, g, :], rhs=self.pooledT[:, tsl], start=True, stop=True),
                 reads=["poolw", "pooledT"], writes=[("ps", b)])
            s.op("act", lambda: nc.scalar.activation(out=self.mixT[:, g, tsl], in_=self.ps[b][:], func=AF.Identity,
                                                     scale=self.pscale[:, g, l:l + 1]),
                 reads=[("ps", b), "pscale"], writes=[("mixT", g, nb)])

    def hgrn_head(self, l, h, sF, sQ, sV, sG):
        nc, s = self.nc, self.s
        hi = h % 2
        cs = slice(hi * 128, (hi + 1) * 128)
        T0, T1, T2, T3 = [self.body(t) for t in self.tb]
        for nb in range(NBLK):
            b = self.proj_fm(sF, cs, nb)
            s.op("act", lambda: nc.scalar.activation(out=T0[:, nb * 512:(nb + 1) * 512], in_=self.ps[b][:], func=AF.Exp, scale=-1.0),
                 reads=[("ps", b)], writes=["t0"])
        s.op("act", lambda: nc.scalar.activation(out=T1, in_=T0, func=AF.Ln, bias=1.0, scale=self.LB[:, h, l:l + 1]),
             reads=["t0", "LB"], writes=["t1"])
        s.op("act", lambda: nc.scalar.activation(out=T2, in_=T0, func=AF.Ln, bias=1.0, scale=1.0), reads=["t0"], writes=["t2"])
        s.op("dve", lambda: nc.vector.tensor_tensor(out=T1, in0=T1, in1=T2, op=ALU.subtract), reads=["t2"], writes=["t1"])
        s.op("act", lambda: nc.scalar.activation(out=T2, in_=T1, func=AF.Exp), reads=["t1"], writes=["t2"])
        s.op("dve", lambda: nc.vector.tensor_scalar(out=T2, in0=T2, scalar1=-1.0, scalar2=1.0, op0=ALU.mult, op1=ALU.add),
             writes=["t2"])
        s.op("dve", lambda: nc.vector.tensor_tensor_scan(out=T0, data0=self.rmask[:], data1=T1, initial=0.0, op0=ALU.mult, op1=ALU.add),
             reads=["rmask", "t1"], writes=["t0"])
        bv = T0.rearrange("p (c j) -> p c j", j=CH)
        s.op("dve", lambda: nc.vector.tensor_tensor(out=T3.rearrange("p (c j) -> p c j", j=CH), in0=bv,
                                                    in1=bv[:, :, CH // 2 - 1:CH // 2].to_broadcast([128, NCH, CH]), op=ALU.subtract),
             reads=["t0"], writes=["t3"])
        s.op("act", lambda: nc.scalar.activation(out=self.e1[:], in_=bv[:, :, CH - 1], func=AF.Exp), reads=["t0"], writes=["e1"])
        s.op("dve", lambda: nc.vector.tensor_tensor(out=self.tmp16[:], in0=bv[:, :, CH - 1], in1=bv[:, :, CH // 2 - 1], op=ALU.subtract),
             reads=["t0"], writes=["tmp16"])
        s.op("act", lambda: nc.scalar.activation(out=self.e2[:], in_=self.tmp16[:], func=AF.Exp), reads=["tmp16"], writes=["e2"])
        s.op("act", lambda: nc.scalar.activation(out=self.Eb[:], in_=bv[:, :, CH // 2 - 1], func=AF.Exp), reads=["t0"], writes=["Eb"])
        s.op("act", lambda: nc.scalar.activation(out=T0, in_=T3, func=AF.Exp), reads=["t3"], writes=["t0"])
        s.op("act", lambda: nc.scalar.activation(out=T3, in_=T3, func=AF.Exp, scale=-1.0), writes=["t3"])
        s.op("dve", lambda: nc.vector.tensor_tensor(out=self.KT[:], in0=T2, in1=T3, op=ALU.mult), reads=["t2", "t3"], writes=["KT"])
        self.chk("m1")
        for nb in range(NBLK):
            b = self.proj_fm(sQ, cs, nb)
            tsl = slice(nb * 512, (nb + 1) * 512)
            s.op("dve", lambda: nc.vector.scalar_tensor_tensor(out=self.QT[:, tsl], in0=self.ps[b][:], scalar=128.0 ** -0.5,
                                                              in1=T0[:, tsl], op0=ALU.mult, op1=ALU.mult),
                 reads=[("ps", b), "t0"], writes=["QT"])
        if hi == 0:
            for tt in range(NTT):
                b = self.psum()
                fns = [lambda kc=kc: nc.tensor.matmul(self.ps[b][:, 0:256], lhsT=self.XT[:, kc, tt * 128:(tt + 1) * 128],
                                                      rhs=self.uslot[sV][:, kc, :], start=(kc == 0), stop=(kc == NKC - 1))
                       for kc in range(NKC)]
                s.op("pe", fns, reads=[("us", sV)] + self.xt_keys([tt]), writes=[("ps", b)])
                s.op("act", lambda: nc.scalar.activation(out=self.vtok[:, tt, :], in_=self.ps[b][:, 0:256], func=AF.Copy),
                     reads=[("ps", b)], writes=["vtok"])
        for hf in range(2):
            fns = [lambda j=j: nc.tensor.transpose(self.psb[:, (hf * 4 + j) * 128:(hf * 4 + j + 1) * 128],
                                                   self.KT[:, (hf * 4 + j) * 128:(hf * 4 + j + 1) * 128], self.identb[:])
                   for j in range(4)]
            s.op("pe", fns, reads=["KT", "identb"], writes=["psb"])
            for mi in range(2):
                s.op("act", lambda: nc.scalar.activation(out=self.ktokm[mi][:, hf * 4:(hf + 1) * 4, :],
                                                         in_=self.psb[:, hf * 512:(hf + 1) * 512].rearrange("p (a b) -> p a b", b=128),
                                                         func=AF.Identity, scale=self.rowm[:, mi:mi + 1]),
                     reads=["psb", "rowm"], writes=[("ktokm", mi)])
        self.chk("m2")
        for q4 in range(NCH // 4):
            b = self.psum()
            fns = []
            for j in range(4):
                c = q4 * 4 + j
                tt, half = divmod(c, 2)
                fns.append(lambda j=j, tt=tt, half=half: nc.tensor.matmul(
                    self.ps[b][:, j * 128:(j + 1) * 128], lhsT=self.ktokm[half][:, tt, :], rhs=self.vtok[:, tt, cs], start=True, stop=True))
            s.op("pe", fns, reads=["vtok", ("ktokm", 0), ("ktokm", 1)], writes=[("ps", b)])
            s.op("dve", lambda: nc.vector.tensor_tensor(
                out=self.data1[:, :, 1 + q4 * 4:1 + q4 * 4 + 4].rearrange("p v c -> p c v"),
                in0=self.ps[b][:].rearrange("p (c v) -> p c v", v=128),
                in1=self.e2[:, q4 * 4:q4 * 4 + 4].unsqueeze(2).to_broadcast([128, 4, 128]), op=ALU.mult),
                reads=[("ps", b), "e2"], writes=["data1"])
        s.op("act", lambda: nc.scalar.activation(out=self.data1[:, :, 0], in_=self.Sst[:, l, h, :], func=AF.Copy),
             reads=["Sst"], writes=["data1"])
        s.op("dve", lambda: nc.vector.tensor_copy(out=self.data0[:, :, 1:NCH + 1],
                                                  in_=self.e1[:].unsqueeze(1).to_broadcast([128, 128, NCH])),
             reads=["e1"], writes=["data0"])
        s.op("dve", lambda: nc.vector.tensor_tensor_scan(out=self.Sall[:].rearrange("p v c -> p (v c)"),
                                                         data0=self.data0[:].rearrange("p v c -> p (v c)"),
                                                         data1=self.data1[:].rearrange("p v c -> p (v c)"),
                                                         initial=0.0, op0=ALU.mult, op1=ALU.add),
             reads=["data0", "data1"], writes=["Sall"])
        s.op("act", lambda: nc.scalar.activation(out=self.Sst[:, l, h, :], in_=self.Sall[:, :, NCH], func=AF.Copy),
             reads=["Sall"], writes=["Sst"])
        s.op("dve", lambda: nc.vector.tensor_tensor(out=self.St[:], in0=self.Sall[:, :, 0:NCH].rearrange("p v c -> p c v"),
                                                    in1=self.Eb[:].unsqueeze(2).to_broadcast([128, NCH, 128]), op=ALU.mult),
             reads=["Sall", "Eb"], writes=["St"])
        self.chk("m3")
        for nb in range(NBLK):
            tsl = slice(nb * 512, (nb + 1) * 512)
            b = self.psum()
            fns = [lambda j=j: nc.tensor.matmul(self.ps[b][:, j * 128:(j + 1) * 128],
                                                lhsT=self.KT[:, (nb * 4 + j) * 128:(nb * 4 + j + 1) * 128],
                                                rhs=self.QT[:, (nb * 4 + j) * 128:(nb * 4 + j + 1) * 128], start=True, stop=True)
                   for j in range(4)]
            s.op("pe", fns, reads=["KT", "QT"], writes=[("ps", b)])
            s.op("dve", lambda: nc.vector.tensor_tensor(out=self.AT[:], in0=self.ps[b][:].rearrange("p (a t) -> p a t", t=128),
                                                        in1=self.mask[:].unsqueeze(1).to_broadcast([128, 4, 128]), op=ALU.mult),
                 reads=[("ps", b), "mask"], writes=["AT"])
            bo = self.psum()
            fns = []
            for j in range(4):
                tt = nb * 4 + j
                for cc in range(2):
                    c = tt * 2 + cc
                    reg = self.ps[bo][:, j * 128 + cc * 64:j * 128 + (cc + 1) * 64]
                    fns.append(lambda reg=reg, tt=tt, j=j, cc=cc: nc.tensor.matmul(
                        reg, lhsT=self.vtok[:, tt, cs], rhs=self.AT[:, j, cc * 64:(cc + 1) * 64], start=True, stop=False))
                    fns.append(lambda reg=reg, c=c: nc.tensor.matmul(
                        reg, lhsT=self.St[:, c, :], rhs=self.QT[:, c * 64:(c + 1) * 64], start=False, stop=True))
            s.op("pe", fns, reads=["vtok", "AT", "St", "QT"], writes=[("ps", bo)])
            s.op("act", lambda: nc.scalar.activation(out=self.osq[:], in_=self.ps[bo][:], func=AF.Square), reads=[("ps", bo)], writes=[("sg", 0)])
            br = self.psum()
            s.op("pe", lambda: nc.tensor.matmul(self.ps[br][:], lhsT=self.ones[:], rhs=self.osq[:], start=True, stop=True),
                 reads=["ones", ("sg", 0)], writes=[("ps", br)])
            s.op("act", lambda: nc.scalar.activation(out=self.rsd[:], in_=self.ps[br][:], func=AF.Ln, bias=RMS_EPS, scale=1.0 / 128),
                 reads=[("ps", br)], writes=["rsd"])
            s.op("act", lambda: nc.scalar.activation(out=self.rsd[:], in_=self.rsd[:], func=AF.Exp, scale=-0.5), writes=["rsd"])
            bg = self.proj_fm(sG, cs, nb)
            s.op("act", lambda: nc.scalar.activation(out=self.sgate[:], in_=self.ps[bg][:], func=AF.Silu), reads=[("ps", bg)], writes=[("sg", 1)])
            s.op("dve", lambda: nc.vector.scalar_tensor_tensor(out=self.ytmp[:], in0=self.ps[bo][:], scalar=self.gn[:, l:l + 1],
                                                              in1=self.rsd[:], op0=ALU.mult, op1=ALU.mult),
                 reads=[("ps", bo), "gn", "rsd"], writes=["ytmp"])
            s.op("dve", lambda: nc.vector.tensor_tensor(out=self.mixT[:, 4 + h, tsl], in0=self.ytmp[:], in1=self.sgate[:], op=ALU.mult),
                 reads=["ytmp", ("sg", 1)], writes=[("mixT", 4 + h, nb)])

    def mixer(self, l):
        nc, s = self.nc, self.s
        Win = self.w_in[l]
        s.dma("pool", self.poolw[:], self.pool_w[l].rearrange("g c d -> c g d"), writes=["poolw"], key="poolw")
        for gp in range(2):
            su = self.load_u(Win[:, gp * 256:(gp + 1) * 256], 256)
            for gi in range(2):
                self.pool_group(l, gp * 2 + gi, su)
        for hp in range(2):
            sF = self.load_u(Win[:, 1024 + hp * 256:1024 + (hp + 1) * 256], 256)
            sQ = self.load_u(Win[:, 512 + hp * 256:512 + (hp + 1) * 256], 256)
            sV = self.load_u(Win[:, 1536 + hp * 256:1536 + (hp + 1) * 256], 256)
            sG = self.load_u(Win[:, 2048 + hp * 256:2048 + (hp + 1) * 256], 256)
            for hi in range(2):
                self.hgrn_head(l, hp * 2 + hi, sF, sQ, sV, sG)
        dsl = [self.load_d(self.w_out[l][kc * 128:(kc + 1) * 128, :]) for kc in range(NKC)]
        for tt in range(NTT):
            for hh in range(2):
                b = self.psum()
                fns = [lambda kc=kc: nc.tensor.matmul(self.ps[b][:], lhsT=self.mixT[:, kc, tt * 128:(tt + 1) * 128],
                                                      rhs=self.dslot[dsl[kc]][:, hh * 512:(hh + 1) * 512],
                                                      start=(kc == 0), stop=(kc == NKC - 1)) for kc in range(NKC)]
                s.op("pe", fns, reads=[("mixT", kc, tt // 4) for kc in range(NKC)] + [("ds", dsl[kc]) for kc in range(NKC)],
                     writes=[("ps", b)])
                xs = self.X[:, tt, hh * 512:(hh + 1) * 512]
                s.op("dve", lambda: nc.vector.scalar_tensor_tensor(out=xs, in0=xs, scalar=ALPHA, in1=self.ps[b][:],
                                                                  op0=ALU.mult, op1=ALU.add),
                     reads=[("ps", b)], writes=[("X", tt, hh)])


def consts():
    ident = np.eye(128, dtype=np.float32)
    s_idx = np.arange(128)[:, None]
    t_idx = np.arange(128)[None, :]
    mask = ((s_idx // CH == t_idx // CH) & (s_idx <= t_idx)).astype(np.float32)
    rc = np.zeros((128, 64), np.float32)
    for g, w in enumerate(POOL_W):
        rc[:, g * 16:(g + 1) * 16] = 1.0 / np.minimum(np.arange(16) + 1, w)
    rowm = np.zeros((128, 2), np.float32)
    rowm[:64, 0] = 1.0
    rowm[64:, 1] = 1.0
    return {"c_ident": ident, "c_mask": mask, "c_rc": rc, "c_rowm": rowm}


_CACHE = {}


def kernel(**inputs):
    x = np.asarray(inputs["x"], dtype=np.float32)
    B, S, _ = x.shape
    nseg = S // TS
    if "nc" not in _CACHE:
        _CACHE["nc"] = Prog(nseg=nseg).build()
    nc = _CACHE["nc"]
    shared = {k: np.ascontiguousarray(np.asarray(v, dtype=np.float32)) for k, v in inputs.items() if k != "x"}
    shared.update(consts())
    in_maps = []
    for b in range(B):
        m = dict(shared)
        m["x"] = np.ascontiguousarray(x[b])
        in_maps.append(m)
    res = run_bass_kernel_spmd(nc, in_maps, core_ids=list(range(B)))
    return np.stack([np.asarray(r["out"], dtype=np.float32) for r in res.results], axis=0)
```

```python
import contextlib
import numpy as np
import concourse.bass as bass
import concourse.mybir as mybir
from concourse.bass_utils import run_bass_kernel_spmd

F32 = mybir.dt.float32
BF16 = mybir.dt.bfloat16
AF = mybir.ActivationFunctionType
ALU = mybir.AluOpType

D = 1024
DFF = 2816
NFC = DFF // 128
NKC = D // 128
DEPTH = 4
SEQ = 4096
TS = 1024
NTT = TS // 128
NBLK = TS // 512
CH = 64
NCH = TS // CH
ALPHA = (2.0 * DEPTH) ** 0.25
LN_EPS = 1e-5
RMS_EPS = 1e-6
POOL_W = (2, 4, 8, 16)
PH = [(0, 6), (6, 12), (12, 17), (17, 22)]


class Ev:
    __slots__ = ("sem", "val", "eng")

    def __init__(self, sem, val, eng):
        self.sem, self.val, self.eng = sem, val, eng


class Buf:
    __slots__ = ("w", "r")

    def __init__(self):
        self.w = None
        self.r = {}


class Sched:
    def __init__(self, nc, stack):
        self.nc = nc
        self.stack = stack
        self.engs = {"pe": nc.tensor, "act": nc.scalar, "dve": nc.vector, "pool": nc.gpsimd, "sp": nc.sync}
        self.sem = {}
        self.cnt = {}
        self.seen = {e: {} for e in self.engs}
        self.bufs = {}
        self.dsem = {}
        self.nsem = 0
        self.last_dma = []
        self.new_epoch()

    def _newsem(self, name):
        self.nsem += 1
        return self.stack.enter_context(self.nc.semaphore(f"{name}_{self.nsem}"))

    def new_epoch(self):
        for e in ("pe", "act", "dve", "pool"):
            self.sem[e] = self._newsem("s" + e)
            self.cnt[e] = 0

    def _buf(self, k):
        b = self.bufs.get(k)
        if b is None:
            b = self.bufs[k] = Buf()
        return b

    def _wait(self, eng, ev):
        if ev is None:
            return
        if eng == "pe" and ev.eng == "pe":
            return
        key = id(ev.sem)
        if self.seen[eng].get(key, 0) >= ev.val:
            return
        self.engs[eng].wait_ge(ev.sem, ev.val)
        self.seen[eng][key] = ev.val

    def _deps(self, eng, reads, writes):
        for k in reads:
            self._wait(eng, self._buf(k).w)
        for k in writes:
            b = self._buf(k)
            self._wait(eng, b.w)
            for ev in b.r.values():
                self._wait(eng, ev)

    def _commit(self, ev, reads, writes):
        for k in reads:
            self._buf(k).r[ev.eng] = ev
        for k in writes:
            b = self._buf(k)
            b.w = ev
            b.r = {}

    def op(self, eng, fns, reads=(), writes=()):
        if callable(fns):
            fns = [fns]
        self._deps(eng, reads, writes)
        inst = None
        for f in fns:
            inst = f()
        self.cnt[eng] += 1
        inst.then_inc(self.sem[eng], 1)
        ev = Ev(self.sem[eng], self.cnt[eng], eng)
        self._commit(ev, reads, writes)
        return ev

    def dma(self, q, out, in_, reads=(), writes=(), key=None, **kw):
        self._deps(q, reads, writes)
        if key not in self.dsem:
            self.dsem[key] = [self._newsem("d"), 0]
        st = self.dsem[key]
        st[1] += 16
        self.engs[q].dma_start(out=out, in_=in_, **kw).then_inc(st[0], 16)
        ev = Ev(st[0], st[1], "dma:" + str(key))
        self._commit(ev, reads, writes)
        return ev

    def coll(self, fn, reads=(), writes=(), key=None):
        self._deps("pool", reads, writes)
        sem = self._newsem("cc")
        fn().then_inc(sem)
        ev = Ev(sem, 1, "cc:" + str(key))
        self._commit(ev, reads, writes)
        return ev

    def wait_all(self, eng, keys):
        for k in keys:
            b = self._buf(k)
            self._wait(eng, b.w)
            for ev in b.r.values():
                self._wait(eng, ev)


class _Stop(Exception):
    pass


class Prog:
    def __init__(self, nseg, nlayers=DEPTH, stop_after=None, skip_ffn=False, pipelined=False):
        self.skip_ffn = skip_ffn
        self.pipelined = pipelined
        self.NL = nlayers
        self.nseg = nseg
        self.nlayers = nlayers
        self.stop_after = stop_after
        self.nc = bass.Bass("TRN2", target_bir_lowering=False)
        self.stack = contextlib.ExitStack()

    def sb(self, name, shape, dt):
        return self.stack.enter_context(self.nc.sbuf_tensor(name, shape, dt))

    def din(self, name, shape, dt=F32):
        return self.nc.dram_tensor(name, shape, dt, kind="ExternalInput").ap()

    def build(self):
        nc = self.nc
        L = self.NL
        S = self.nseg * TS
        with self.stack:
            self.s = Sched(nc, self.stack)
            self.x = self.din("x", [S, D])
            self.w_in = self.din("w_in", [L, D, 2560])
            self.pool_w = self.din("pool_w", [L, 4, 128, 128])
            self.pool_scale = self.din("pool_scale", [L, 512])
            self.lb_param = self.din("lb_param", [DEPTH, 512])
            self.hgn = self.din("hgrn_norm_g", [L, 128])
            self.w_out = self.din("w_out", [L, D, D])
            self.fg = [self.din("ffn1_gate", [L, D, DFF]), self.din("ffn2_gate", [L, D, DFF])]
            self.fu = [self.din("ffn1_up", [L, D, DFF]), self.din("ffn2_up", [L, D, DFF])]
            self.fd = [self.din("ffn1_down", [L, DFF, D]), self.din("ffn2_down", [L, DFF, D])]
            self.ln_g = self.din("ln_g", [L, 3, D])
            self.ln_b = self.din("ln_b", [L, 3, D])
            self.c_ident = self.din("c_ident", [128, 128])
            self.c_mask = self.din("c_mask", [128, 128])
            self.c_rc = self.din("c_rc", [128, 2, 64])
            self.c_lbmask = self.din("c_lbmask", [128, L, DEPTH])
            self.c_sel = self.din("c_sel", [128, 2])
            self.c_rowm = self.din("c_rowm", [128, 2])
            self.out = nc.dram_tensor("out", [S, D], F32, kind="ExternalOutput").ap()
            if self.pipelined:
                HS = TS // 2
                self.cc_in = [nc.dram_tensor(f"cc_in{i}", [HS, D], F32).ap() for i in range(2)]
                self.cc_out = [nc.dram_tensor(f"cc_out{i}", [2 * HS, D], F32).ap() for i in range(2)]

            self.X = self.sb("X", [128, NTT, D], F32)
            self.XT = self.sb("XT", [128, NKC, TS], BF16)
            self.ACTT = self.sb("ACTT", [128, 6, TS], BF16)
            self.NU = 4
            self.uslot = [self.sb(f"us{i}", [128, NKC, 256], BF16) for i in range(self.NU)]
            self.ND = 8
            self.dslot = [self.sb(f"ds{i}", [128, D], BF16) for i in range(self.ND)]
            self.ui = 0
            self.di = 0
            self.Gt = self.sb("Gt", [128, D], F32)
            self.Bt = self.sb("Bt", [128, D], F32)
            self.sg = [self.sb(f"sg{i}", [128, 512], F32) for i in range(2)]
            self.sgi = 0
            self.ident = self.sb("ident", [128, 128], F32)
            self.stats = self.sb("stats", [128, NTT, 2, 6], F32)
            self.mv = self.sb("mv", [128, NTT, 2], F32)
            self.lnt = self.sb("lnt", [128, NTT], F32)
            self.rstd = self.sb("rstd", [128, NTT], F32)
            self.nmr = self.sb("nmr", [128, NTT], F32)
            self.NPS = 7
            self.ps = [self.stack.enter_context(nc.psum_tensor(f"ps{i}", [128, 512], F32)) for i in range(self.NPS)]
            self.psb = self.stack.enter_context(nc.psum_tensor("psb", [128, 1024], BF16))
            self.psi = 0
            self.alloc_mixer()

            self.emit()
        return nc

    def chk(self, name):
        if self.stop_after == name:
            raise _Stop()

    def psum(self):
        i = self.psi
        self.psi = (self.psi + 1) % self.NPS
        return i

    def load_u(self, src, ncols):
        i = self.ui
        self.ui = (self.ui + 1) % self.NU
        t = self.uslot[i]
        self.s.dma("pool", t[:, :, 0:ncols], src.rearrange("(kc p) n -> p kc n", p=128),
                   writes=[("us", i)], key=("us", i))
        return i

    def load_d(self, src):
        i = self.di
        self.di = (self.di + 1) % self.ND
        self.s.dma("pool", self.dslot[i][:], src, writes=[("ds", i)], key=("ds", i))
        return i

    def emit(self):
        nc, s = self.nc, self.s
        s.dma("sp", self.ident[:], self.c_ident, writes=["ident"], key="ident")
        self.setup()
        nsteps = self.nseg + (1 if self.pipelined else 0)
        xkeys = lambda tt: [("X", tt, 0), ("X", tt, 1)]
        for t in range(nsteps):
            self.seg = t
            seg_in = min(t, self.nseg - 1)
            t0 = seg_in * TS
            for tt in range(NTT):
                s.dma("sp", self.X[:, tt, :], self.x[t0 + tt * 128: t0 + (tt + 1) * 128, :],
                      writes=xkeys(tt), key=("xin", tt))
            if self.pipelined:
                for tt in range(NTT):
                    xs = self.X[:, tt, :]
                    s.op("dve", lambda: nc.vector.tensor_scalar(out=xs, in0=xs, scalar1=self.sel[:, 0:1], scalar2=None, op0=ALU.mult),
                         reads=["sel"], writes=xkeys(tt))
                    if t >= 1:
                        i = tt % 4
                        stg = self.tb[i][:, 0:D]
                        s.dma("sp", stg, self.cc_out[tt // 4][(tt % 4) * 128:(tt % 4 + 1) * 128, :], reads=[("ccout", tt // 4)], writes=[f"t{i}"], key=("stg", i))
                        s.op("dve", lambda: nc.vector.scalar_tensor_tensor(out=xs, in0=stg, scalar=self.sel[:, 1:2], in1=xs,
                                                                          op0=ALU.mult, op1=ALU.add),
                             reads=[f"t{i}", "sel"], writes=xkeys(tt))
            for tt in range(NTT):
                self.transpose_tile(tt)
            try:
                for l in range(self.nlayers):
                    self.chk("setup")
                    if l % 2 == 0:
                        s.new_epoch()
                    if not self.skip_ffn:
                        self.ffn(l, 0)
                        self.chk("ffn1")
                        self.layernorm(l, 0)
                        self.chk("ln0")
                    self.mixer(l)
                    self.chk("mix")
                    self.layernorm(l, 1)
                    self.chk("ln1")
                    self.ffn(l, 1)
                    self.layernorm(l, 2)
            except _Stop:
                pass
            if not self.pipelined:
                for tt in range(NTT):
                    s.dma("sp", self.out[t0 + tt * 128: t0 + (tt + 1) * 128, :], self.X[:, tt, :],
                          reads=xkeys(tt), key=("xout", tt))
                continue
            if t >= 1:
                o0 = (t - 1) * TS
                for tt in range(NTT):
                    s.dma("sp", self.out[o0 + tt * 128: o0 + (tt + 1) * 128, :], self.X[:, tt, :],
                          reads=xkeys(tt), key=("xout", tt))
            if t < self.nseg:
                for hf in range(2):
                    for tt in range(hf * 4, hf * 4 + 4):
                        s.dma("sp", self.cc_in[hf][(tt % 4) * 128:(tt % 4 + 1) * 128, :], self.X[:, tt, :],
                              reads=xkeys(tt), writes=[("ccin", tt)], key=("ccin", tt))
                    s.coll(lambda: nc.gpsimd.collective_compute(
                        "AllGather", ALU.bypass, replica_groups=[[0, 1], [2, 3], [4, 5], [6, 7]],
                        ins=[self.cc_in[hf].opt()], outs=[self.cc_out[hf].opt()]),
                        reads=[("ccin", tt) for tt in range(hf * 4, hf * 4 + 4)], writes=[("ccout", hf)], key=("cc", t, hf))
            if t == 0:
                s.op("dve", lambda: nc.vector.tensor_scalar(out=self.Sst[:], in0=self.Sst[:], scalar1=self.sel[:, 0:1], scalar2=None, op0=ALU.mult),
                     reads=["sel"], writes=["Sst"])
                s.op("dve", lambda: nc.vector.tensor_scalar(out=self.utail[:], in0=self.utail[:], scalar1=self.sel[:, 0:1], scalar2=None, op0=ALU.mult),
                     reads=["sel"], writes=["utail"])
        for tt in range(NTT):
            st = s.dsem[("xout", tt)]
            nc.sync.wait_ge(st[0], st[1])

    def transpose_tile(self, tt):
        nc, s = self.nc, self.s
        for hh in range(2):
            b = self.psum()
            pt = self.ps[b]
            fns = []
            for j in range(4):
                kc = hh * 4 + j
                fns.append(lambda j=j, kc=kc: nc.tensor.transpose(
                    pt[:, j * 128:(j + 1) * 128], self.X[:, tt, kc * 128:(kc + 1) * 128], self.ident[:]))
            s.op("pe", fns, reads=[("X", tt, hh), "ident"], writes=[("ps", b)])
            s.op("act", lambda: nc.scalar.activation(
                out=self.XT[:, hh * 4:(hh + 1) * 4, tt * 128:(tt + 1) * 128],
                in_=pt[:].rearrange("p (a b) -> p a b", b=128), func=AF.Copy),
                reads=[("ps", b)], writes=[("XT", tt, hh)])

    def xt_keys(self, tts):
        return [("XT", tt, hh) for tt in tts for hh in range(2)]

    def ffn(self, l, which):
        nc, s = self.nc, self.s
        Wg, Wu, Wd = self.fg[which][l], self.fu[which][l], self.fd[which][l]
        for pi, (c0, c1) in enumerate(PH):
            dsl = None
            units = list(range(c0, c1, 2))
            for ui_, cp in enumerate(units):
                nw = min(2, c1 - cp)
                ig = self.load_u(Wg[:, cp * 128:(cp + nw) * 128], nw * 128)
                iu = self.load_u(Wu[:, cp * 128:(cp + nw) * 128], nw * 128)
                if ui_ == 0:
                    dsl = [self.load_d(Wd[c * 128:(c + 1) * 128, :]) for c in range(c0, c1)]
                for ci in range(nw):
                    c = cp + ci
                    for nb in range(NBLK):
                        bg, bu = self.psum(), self.psum()
                        tsl = slice(nb * 512, (nb + 1) * 512)
                        xk = self.xt_keys(range(nb * 4, nb * 4 + 4))
                        fns = [lambda kc=kc: nc.tensor.matmul(
                            self.ps[bg][:], lhsT=self.uslot[ig][:, kc, ci * 128:(ci + 1) * 128],
                            rhs=self.XT[:, kc, tsl], start=(kc == 0), stop=(kc == NKC - 1)) for kc in range(NKC)]
                        s.op("pe", fns, reads=[("us", ig)] + xk, writes=[("ps", bg)])
                        fns = [lambda kc=kc: nc.tensor.matmul(
                            self.ps[bu][:], lhsT=self.uslot[iu][:, kc, ci * 128:(ci + 1) * 128],
                            rhs=self.XT[:, kc, tsl], start=(kc == 0), stop=(kc == NKC - 1)) for kc in range(NKC)]
                        s.op("pe", fns, reads=[("us", iu)] + xk, writes=[("ps", bu)])
                        si = self.sgi
                        self.sgi ^= 1
                        s.op("act", lambda: nc.scalar.activation(out=self.sg[si][:], in_=self.ps[bg][:], func=AF.Silu),
                             reads=[("ps", bg)], writes=[("sg", si)])
                        s.op("dve", lambda: nc.vector.scalar_tensor_tensor(
                            out=self.ACTT[:, c - c0, tsl], in0=self.sg[si][:], scalar=0.5, in1=self.ps[bu][:],
                            op0=ALU.mult, op1=ALU.mult),
                            reads=[("sg", si), ("ps", bu)], writes=[("ACTT", c - c0, nb)])
            ncn = c1 - c0
            for tt in range(NTT):
                for hh in range(2):
                    b = self.psum()
                    fns = [lambda j=j: nc.tensor.matmul(
                        self.ps[b][:], lhsT=self.ACTT[:, j, tt * 128:(tt + 1) * 128],
                        rhs=self.dslot[dsl[j]][:, hh * 512:(hh + 1) * 512], start=(j == 0), stop=(j == ncn - 1))
                        for j in range(ncn)]
                    s.op("pe", fns, reads=[("ACTT", j, tt // 4) for j in range(ncn)] + [("ds", dsl[j]) for j in range(ncn)],
                         writes=[("ps", b)])
                    xs = self.X[:, tt, hh * 512:(hh + 1) * 512]
                    if pi == 0:
                        s.op("dve", lambda: nc.vector.scalar_tensor_tensor(
                            out=xs, in0=xs, scalar=ALPHA, in1=self.ps[b][:], op0=ALU.mult, op1=ALU.add),
                            reads=[("ps", b)], writes=[("X", tt, hh)])
                    else:
                        s.op("dve", lambda: nc.vector.tensor_tensor(out=xs, in0=xs, in1=self.ps[b][:], op=ALU.add),
                             reads=[("ps", b)], writes=[("X", tt, hh)])

    def layernorm(self, l, j):
        nc, s = self.nc, self.s
        s.dma("sp", self.Gt[:], self.ln_g[l, j:j + 1, :].partition_broadcast(128), writes=["Gt"], key="Gt")
        s.dma("sp", self.Bt[:], self.ln_b[l, j:j + 1, :].partition_broadcast(128), writes=["Bt"], key="Bt")
        for tt in range(NTT):
            for hh in range(2):
                s.op("dve", lambda: nc.vector.bn_stats(out=self.stats[:, tt, hh, :], in_=self.X[:, tt, hh * 512:(hh + 1) * 512]),
                     reads=[("X", tt, hh)], writes=[("stats", tt, hh)])
            s.op("dve", lambda: nc.vector.bn_aggr(out=self.mv[:, tt, :], in_=self.stats[:, tt, :, :]),
                 reads=[("stats", tt, 0), ("stats", tt, 1)], writes=[("mv", tt)])
        mvk = [("mv", tt) for tt in range(NTT)]
        s.op("act", lambda: nc.scalar.activation(out=self.lnt[:], in_=self.mv[:, :, 1], func=AF.Ln, bias=LN_EPS, scale=1.0),
             reads=mvk, writes=["lnt"])
        s.op("act", lambda: nc.scalar.activation(out=self.rstd[:], in_=self.lnt[:], func=AF.Exp, scale=-0.5),
             reads=["lnt"], writes=["rstd"])
        s.op("dve", lambda: nc.vector.scalar_tensor_tensor(out=self.nmr[:], in0=self.mv[:, :, 0], scalar=-1.0, in1=self.rstd[:],
                                                          op0=ALU.mult, op1=ALU.mult),
             reads=mvk + ["rstd"], writes=["nmr"])
        for tt in range(NTT):
            xk = [("X", tt, 0), ("X", tt, 1)]
            s.op("act", lambda: nc.scalar.activation(out=self.X[:, tt, :], in_=self.X[:, tt, :], func=AF.Identity,
                                                     bias=self.nmr[:, tt:tt + 1], scale=self.rstd[:, tt:tt + 1]),
                 reads=["nmr", "rstd"], writes=xk)
            s.op("dve", lambda: nc.vector.tensor_tensor(out=self.X[:, tt, :], in0=self.X[:, tt, :], in1=self.Gt[:], op=ALU.mult),
                 reads=["Gt"], writes=xk)
            s.op("dve", lambda: nc.vector.tensor_tensor(out=self.X[:, tt, :], in0=self.X[:, tt, :], in1=self.Bt[:], op=ALU.add),
                 reads=["Bt"], writes=xk)
            self.transpose_tile(tt)

    def alloc_mixer(self):
        sb = self.sb
        L = self.NL
        self.tb = [sb(f"tb{i}", [128, 16 + TS], F32) for i in range(4)]
        self.QT = sb("QT", [128, TS], BF16)
        self.KT = sb("KT", [128, TS], BF16)
        self.vtok = sb("vtok", [128, NTT, 256], BF16)
        self.ktokm = [sb(f"ktokm{i}", [128, NTT, 128], BF16) for i in range(2)]
        self.rowm = sb("rowm", [128, 2], F32)
        self.data0 = sb("data0", [128, 128, NCH + 1], F32)
        self.data1 = sb("data1", [128, 128, NCH + 1], F32)
        self.Sall = sb("Sall", [128, 128, NCH + 1], F32)
        self.St = sb("St", [128, NCH, 128], BF16)
        self.AT = sb("AT", [128, 4, 128], BF16)
        self.osq = self.sg[0]
        self.rsd = sb("rsd", [128, 512], F32)
        self.ytmp = sb("ytmp", [128, 512], F32)
        self.sgate = self.sg[1]
        self.pooledT = sb("pooledT", [128, TS], BF16)
        self.mixT = sb("mixT", [128, NKC, TS], BF16)
        self.poolw = sb("poolw", [128, 4, 128], BF16)
        self.e1 = sb("e1", [128, NCH], F32)
        self.e2 = sb("e2", [128, NCH], F32)
        self.Eb = sb("Eb", [128, NCH], F32)
        self.tmp16 = sb("tmp16", [128, NCH], F32)
        self.Sst = sb("Sst", [128, L, 4, 128], F32)
        self.utail = sb("utail", [128, L, 4, 16], F32)
        self.LB = sb("LB", [128, 4, L], F32)
        self.lbx = sb("lbx", [128, 4, DEPTH], F32)
        self.lbt = sb("lbt", [128, 4, DEPTH], F32)
        self.lbm = sb("lbm", [128, L, DEPTH], F32)
        self.sel = sb("sel", [128, 2], F32)
        self.lbs = sb("lbs", [128, 4], F32)
        self.gn = sb("gn", [128, L], F32)
        self.pscale = sb("pscale", [128, 4, L], F32)
        self.mask = sb("mask", [128, 128], F32)
        self.rc = sb("rc", [128, 2, 64], F32)
        self.ones = sb("ones", [128, 128], F32)
        self.identb = sb("identb", [128, 128], BF16)
        self.rmask = sb("rmask", [128, TS], F32)

    def setup(self):
        nc, s = self.nc, self.s
        s.dma("sp", self.mask[:], self.c_mask, writes=["mask"], key="mask")
        s.dma("sp", self.rc[:], self.c_rc, writes=["rc"], key="rc")
        s.dma("sp", self.rowm[:], self.c_rowm, writes=["rowm"], key="rowm")
        s.dma("sp", self.lbm[:], self.c_lbmask, writes=["lbm"], key="lbm")
        s.dma("sp", self.sel[:], self.c_sel, writes=["sel"], key="sel")
        for h in range(4):
            s.dma("sp", self.lbx[:, h, :], self.lb_param[:, h * 128:(h + 1) * 128].rearrange("l d -> d l"), writes=["lbx"],
                  key=("lbx", h), allow_slow_non_contiguous=True)
        s.dma("sp", self.gn[:], self.hgn.rearrange("l d -> d l"), writes=["gn"], key="gn", allow_slow_non_contiguous=True)
        for g in range(4):
            s.dma("sp", self.pscale[:, g, :], self.pool_scale[:, g * 128:(g + 1) * 128].rearrange("l d -> d l"), writes=["pscale"],
                  key=("pscale", g), allow_slow_non_contiguous=True)
        s.op("dve", lambda: nc.vector.memset(self.ones[:], 1.0), writes=["ones"])
        s.op("dve", lambda: nc.vector.tensor_copy(out=self.identb[:], in_=self.ident[:]), reads=["ident"], writes=["identb"])
        s.op("dve", lambda: nc.vector.memset(self.rmask[:], 1.0), writes=["rmask"])
        s.op("dve", lambda: nc.vector.memset(self.rmask[:].rearrange("p (c j) -> p c j", j=CH)[:, :, 0:1], 0.0), writes=["rmask"])
        s.op("dve", lambda: nc.vector.memset(self.data0[:], 0.0), writes=["data0"])
        s.op("dve", lambda: nc.vector.memset(self.Sst[:], 0.0), writes=["Sst"])
        s.op("dve", lambda: nc.vector.memset(self.utail[:], 0.0), writes=["utail"])
        s.op("dve", lambda: nc.vector.tensor_reduce(out=self.lbs[:], in_=self.lbx[:], axis=mybir.AxisListType.X, op=ALU.max),
             reads=["lbx"], writes=["lbs"])
        s.op("dve", lambda: nc.vector.tensor_tensor(out=self.lbx[:], in0=self.lbx[:],
                                                    in1=self.lbs[:].unsqueeze(2).to_broadcast([128, 4, DEPTH]), op=ALU.subtract),
             reads=["lbs"], writes=["lbx"])
        s.op("act", lambda: nc.scalar.activation(out=self.lbx[:], in_=self.lbx[:], func=AF.Exp), writes=["lbx"])
        s.op("dve", lambda: nc.vector.tensor_reduce(out=self.lbs[:], in_=self.lbx[:], axis=mybir.AxisListType.X, op=ALU.add),
             reads=["lbx"], writes=["lbs"])
        s.op("dve", lambda: nc.vector.reciprocal(out=self.lbs[:], in_=self.lbs[:]), writes=["lbs"])
        for j in range(self.NL):
            s.op("dve", lambda: nc.vector.tensor_tensor(out=self.lbt[:], in0=self.lbx[:],
                                                        in1=self.lbm[:, j, :].unsqueeze(1).to_broadcast([128, 4, DEPTH]), op=ALU.mult),
                 reads=["lbx", "lbm"], writes=["lbt"])
            s.op("dve", lambda: nc.vector.tensor_reduce(out=self.LB[:, :, j], in_=self.lbt[:], axis=mybir.AxisListType.X, op=ALU.add),
                 reads=["lbt"], writes=["LB"])
        s.op("dve", lambda: nc.vector.tensor_tensor(out=self.LB[:], in0=self.LB[:],
                                                    in1=self.lbs[:].unsqueeze(2).to_broadcast([128, 4, self.NL]), op=ALU.mult),
             reads=["lbs"], writes=["LB"])

    def proj_fm(self, slot, cs, nb, extra_reads=()):
        nc, s = self.nc, self.s
        b = self.psum()
        tsl = slice(nb * 512, (nb + 1) * 512)
        fns = [lambda kc=kc: nc.tensor.matmul(self.ps[b][:], lhsT=self.uslot[slot][:, kc, cs], rhs=self.XT[:, kc, tsl],
                                              start=(kc == 0), stop=(kc == NKC - 1)) for kc in range(NKC)]
        s.op("pe", fns, reads=[("us", slot)] + self.xt_keys(range(nb * 4, nb * 4 + 4)) + list(extra_reads), writes=[("ps", b)])
        return b

    def body(self, t):
        return t[:, 16:16 + TS]

    def pool_group(self, l, g, slot):
        nc, s = self.nc, self.s
        gi = g % 2
        cs = slice(gi * 128, (gi + 1) * 128)
        T0, T1, T2, T3 = self.tb
        for nb in range(NBLK):
            b = self.proj_fm(slot, cs, nb)
            s.op("act", lambda: nc.scalar.activation(out=T0[:, 16 + nb * 512:16 + (nb + 1) * 512], in_=self.ps[b][:], func=AF.Copy),
                 reads=[("ps", b)], writes=["t0"])
        s.op("act", lambda: nc.scalar.activation(out=T0[:, 0:16], in_=self.utail[:, l, g, :], func=AF.Copy),
             reads=["utail"], writes=["t0"])
        s.op("act", lambda: nc.scalar.activation(out=self.utail[:, l, g, :], in_=T0[:, TS:TS + 16], func=AF.Copy),
             reads=["t0"], writes=["utail"])
        cur, curk, lo = T0, "t0", 0
        for k in range(g + 1):
            sh = 1 << k
            lo2 = lo + sh
            dst, dstk = (T1, "t1") if cur is not T1 else (T2, "t2")
            s.op("dve", lambda: nc.vector.tensor_tensor(out=dst[:, lo2:16 + TS], in0=cur[:, lo2:16 + TS],
                                                        in1=cur[:, lo2 - sh:16 + TS - sh], op=ALU.add),
                 reads=[curk], writes=[dstk])
            cur, curk, lo = dst, dstk, lo2
        w = POOL_W[g]
        s.op("dve", lambda: nc.vector.scalar_tensor_tensor(out=self.pooledT[:], in0=self.body(cur), scalar=1.0 / w,
                                                          in1=self.body(T0), op0=ALU.mult, op1=ALU.subtract),
             reads=[curk, "t0"], writes=["pooledT"])
        if self.seg <= (1 if self.pipelined else 0):
            s.op("dve", lambda: nc.vector.tensor_tensor(out=T3[:, 0:16], in0=cur[:, 16:32], in1=self.rc[:, self.seg, g * 16:(g + 1) * 16], op=ALU.mult),
                 reads=[curk, "rc"], writes=["t3"])
            s.op("dve", lambda: nc.vector.tensor_tensor(out=self.pooledT[:, 0:16], in0=T3[:, 0:16], in1=T0[:, 16:32], op=ALU.subtract),
                 reads=["t3", "t0"], writes=["pooledT"])
        for nb in range(NBLK):
            b = self.psum()
            tsl = slice(nb * 512, (nb + 1) * 512)
            s.op("pe", lambda: nc.tensor.matmul(self.ps[b][:], lhsT=self.poolw[:, g, :], rhs=self.pooledT[:, tsl], start=True, stop=True),
                 reads=["poolw", "pooledT"], writes=[("ps", b)])
            s.op("act", lambda: nc.scalar.activation(out=self.mixT[:, g, tsl], in_=self.ps[b][:], func=AF.Identity,
                                                     scale=self.pscale[:, g, l:l + 1]),
                 reads=[("ps", b), "pscale"], writes=[("mixT", g, nb)])

    def hgrn_head(self, l, h, sF, sQ, sV, sG):
        nc, s = self.nc, self.s
        hi = h % 2
        cs = slice(hi * 128, (hi + 1) * 128)
        T0, T1, T2, T3 = [self.body(t) for t in self.tb]
        for nb in range(NBLK):
            b = self.proj_fm(sF, cs, nb)
            s.op("act", lambda: nc.scalar.activation(out=T0[:, nb * 512:(nb + 1) * 512], in_=self.ps[b][:], func=AF.Exp, scale=-1.0),
                 reads=[("ps", b)], writes=["t0"])
        s.op("act", lambda: nc.scalar.activation(out=T1, in_=T0, func=AF.Ln, bias=1.0, scale=self.LB[:, h, l:l + 1]),
             reads=["t0", "LB"], writes=["t1"])
        s.op("act", lambda: nc.scalar.activation(out=T2, in_=T0, func=AF.Ln, bias=1.0, scale=1.0), reads=["t0"], writes=["t2"])
        s.op("dve", lambda: nc.vector.tensor_tensor(out=T1, in0=T1, in1=T2, op=ALU.subtract), reads=["t2"], writes=["t1"])
        s.op("act", lambda: nc.scalar.activation(out=T2, in_=T1, func=AF.Exp), reads=["t1"], writes=["t2"])
        s.op("dve", lambda: nc.vector.tensor_scalar(out=T2, in0=T2, scalar1=-1.0, scalar2=1.0, op0=ALU.mult, op1=ALU.add),
             writes=["t2"])
        s.op("dve", lambda: nc.vector.tensor_tensor_scan(out=T0, data0=self.rmask[:], data1=T1, initial=0.0, op0=ALU.mult, op1=ALU.add),
             reads=["rmask", "t1"], writes=["t0"])
        bv = T0.rearrange("p (c j) -> p c j", j=CH)
        s.op("dve", lambda: nc.vector.tensor_tensor(out=T3.rearrange("p (c j) -> p c j", j=CH), in0=bv,
                                                    in1=bv[:, :, CH // 2 - 1:CH // 2].to_broadcast([128, NCH, CH]), op=ALU.subtract),
             reads=["t0"], writes=["t3"])
        s.op("act", lambda: nc.scalar.activation(out=self.e1[:], in_=bv[:, :, CH - 1], func=AF.Exp), reads=["t0"], writes=["e1"])
        s.op("dve", lambda: nc.vector.tensor_tensor(out=self.tmp16[:], in0=bv[:, :, CH - 1], in1=bv[:, :, CH // 2 - 1], op=ALU.subtract),
             reads=["t0"], writes=["tmp16"])
        s.op("act", lambda: nc.scalar.activation(out=self.e2[:], in_=self.tmp16[:], func=AF.Exp), reads=["tmp16"], writes=["e2"])
        s.op("act", lambda: nc.scalar.activation(out=self.Eb[:], in_=bv[:, :, CH // 2 - 1], func=AF.Exp), reads=["t0"], writes=["Eb"])
        s.op("act", lambda: nc.scalar.activation(out=T0, in_=T3, func=AF.Exp), reads=["t3"], writes=["t0"])
        s.op("act", lambda: nc.scalar.activation(out=T3, in_=T3, func=AF.Exp, scale=-1.0), writes=["t3"])
        s.op("dve", lambda: nc.vector.tensor_tensor(out=self.KT[:], in0=T2, in1=T3, op=ALU.mult), reads=["t2", "t3"], writes=["KT"])
        self.chk("m1")
        for nb in range(NBLK):
            b = self.proj_fm(sQ, cs, nb)
            tsl = slice(nb * 512, (nb + 1) * 512)
            s.op("dve", lambda: nc.vector.scalar_tensor_tensor(out=self.QT[:, tsl], in0=self.ps[b][:], scalar=128.0 ** -0.5,
                                                              in1=T0[:, tsl], op0=ALU.mult, op1=ALU.mult),
                 reads=[("ps", b), "t0"], writes=["QT"])
        if hi == 0:
            for tt in range(NTT):
                b = self.psum()
                fns = [lambda kc=kc: nc.tensor.matmul(self.ps[b][:, 0:256], lhsT=self.XT[:, kc, tt * 128:(tt + 1) * 128],
                                                      rhs=self.uslot[sV][:, kc, :], start=(kc == 0), stop=(kc == NKC - 1))
                       for kc in range(NKC)]
                s.op("pe", fns, reads=[("us", sV)] + self.xt_keys([tt]), writes=[("ps", b)])
                s.op("act", lambda: nc.scalar.activation(out=self.vtok[:, tt, :], in_=self.ps[b][:, 0:256], func=AF.Copy),
                     reads=[("ps", b)], writes=["vtok"])
        for hf in range(2):
            fns = [lambda j=j: nc.tensor.transpose(self.psb[:, (hf * 4 + j) * 128:(hf * 4 + j + 1) * 128],
                                                   self.KT[:, (hf * 4 + j) * 128:(hf * 4 + j + 1) * 128], self.identb[:])
                   for j in range(4)]
            s.op("pe", fns, reads=["KT", "identb"], writes=["psb"])
            for mi in range(2):
                s.op("act", lambda: nc.scalar.activation(out=self.ktokm[mi][:, hf * 4:(hf + 1) * 4, :],
                                                         in_=self.psb[:, hf * 512:(hf + 1) * 512].rearrange("p (a b) -> p a b", b=128),
                                                         func=AF.Identity, scale=self.rowm[:, mi:mi + 1]),
                     reads=["psb", "rowm"], writes=[("ktokm", mi)])
        self.chk("m2")
        for q4 in range(NCH // 4):
            b = self.psum()
            fns = []
            for j in range(4):
                c = q4 * 4 + j
                tt, half = divmod(c, 2)
                fns.append(lambda j=j, tt=tt, half=half: nc.tensor.matmul(
                    self.ps[b][:, j * 128:(j + 1) * 128], lhsT=self.ktokm[half][:, tt, :], rhs=self.vtok[:, tt, cs], start=True, stop=True))
            s.op("pe", fns, reads=["vtok", ("ktokm", 0), ("ktokm", 1)], writes=[("ps", b)])
            s.op("dve", lambda: nc.vector.tensor_tensor(
                out=self.data1[:, :, 1 + q4 * 4:1 + q4 * 4 + 4].rearrange("p v c -> p c v"),
                in0=self.ps[b][:].rearrange("p (c v) -> p c v", v=128),
                in1=self.e2[:, q4 * 4:q4 * 4 + 4].unsqueeze(2).to_broadcast([128, 4, 128]), op=ALU.mult),
                reads=[("ps", b), "e2"], writes=["data1"])
        s.op("act", lambda: nc.scalar.activation(out=self.data1[:, :, 0], in_=self.Sst[:, l, h, :], func=AF.Copy),
             reads=["Sst"], writes=["data1"])
        s.op("dve", lambda: nc.vector.tensor_copy(out=self.data0[:, :, 1:NCH + 1],
                                                  in_=self.e1[:].unsqueeze(1).to_broadcast([128, 128, NCH])),
             reads=["e1"], writes=["data0"])
        s.op("dve", lambda: nc.vector.tensor_tensor_scan(out=self.Sall[:].rearrange("p v c -> p (v c)"),
                                                         data0=self.data0[:].rearrange("p v c -> p (v c)"),
                                                         data1=self.data1[:].rearrange("p v c -> p (v c)"),
                                                         initial=0.0, op0=ALU.mult, op1=ALU.add),
             reads=["data0", "data1"], writes=["Sall"])
        s.op("act", lambda: nc.scalar.activation(out=self.Sst[:, l, h, :], in_=self.Sall[:, :, NCH], func=AF.Copy),
             reads=["Sall"], writes=["Sst"])
        s.op("dve", lambda: nc.vector.tensor_tensor(out=self.St[:], in0=self.Sall[:, :, 0:NCH].rearrange("p v c -> p c v"),
                                                    in1=self.Eb[:].unsqueeze(2).to_broadcast([128, NCH, 128]), op=ALU.mult),
             reads=["Sall", "Eb"], writes=["St"])
        self.chk("m3")
        for nb in range(NBLK):
            tsl = slice(nb * 512, (nb + 1) * 512)
            b = self.psum()
            fns = [lambda j=j: nc.tensor.matmul(self.ps[b][:, j * 128:(j + 1) * 128],
                                                lhsT=self.KT[:, (nb * 4 + j) * 128:(nb * 4 + j + 1) * 128],
                                                rhs=self.QT[:, (nb * 4 + j) * 128:(nb * 4 + j + 1) * 128], start=True, stop=True)
                   for j in range(4)]
            s.op("pe", fns, reads=["KT", "QT"], writes=[("ps", b)])
            s.op("dve", lambda: nc.vector.tensor_tensor(out=self.AT[:], in0=self.ps[b][:].rearrange("p (a t) -> p a t", t=128),
                                                        in1=self.mask[:].unsqueeze(1).to_broadcast([128, 4, 128]), op=ALU.mult),
                 reads=[("ps", b), "mask"], writes=["AT"])
            bo = self.psum()
            fns = []
            for j in range(4):
                tt = nb * 4 + j
                for cc in range(2):
                    c = tt * 2 + cc
                    reg = self.ps[bo][:, j * 128 + cc * 64:j * 128 + (cc + 1) * 64]
                    fns.append(lambda reg=reg, tt=tt, j=j, cc=cc: nc.tensor.matmul(
                        reg, lhsT=self.vtok[:, tt, cs], rhs=self.AT[:, j, cc * 64:(cc + 1) * 64], start=True, stop=False))
                    fns.append(lambda reg=reg, c=c: nc.tensor.matmul(
                        reg, lhsT=self.St[:, c, :], rhs=self.QT[:, c * 64:(c + 1) * 64], start=False, stop=True))
            s.op("pe", fns, reads=["vtok", "AT", "St", "QT"], writes=[("ps", bo)])
            s.op("act", lambda: nc.scalar.activation(out=self.osq[:], in_=self.ps[bo][:], func=AF.Square), reads=[("ps", bo)], writes=[("sg", 0)])
            br = self.psum()
            s.op("pe", lambda: nc.tensor.matmul(self.ps[br][:], lhsT=self.ones[:], rhs=self.osq[:], start=True, stop=True),
                 reads=["ones", ("sg", 0)], writes=[("ps", br)])
            s.op("act", lambda: nc.scalar.activation(out=self.rsd[:], in_=self.ps[br][:], func=AF.Ln, bias=RMS_EPS, scale=1.0 / 128),
                 reads=[("ps", br)], writes=["rsd"])
            s.op("act", lambda: nc.scalar.activation(out=self.rsd[:], in_=self.rsd[:], func=AF.Exp, scale=-0.5), writes=["rsd"])
            bg = self.proj_fm(sG, cs, nb)
            s.op("act", lambda: nc.scalar.activation(out=self.sgate[:], in_=self.ps[bg][:], func=AF.Silu), reads=[("ps", bg)], writes=[("sg", 1)])
            s.op("dve", lambda: nc.vector.scalar_tensor_tensor(out=self.ytmp[:], in0=self.ps[bo][:], scalar=self.gn[:, l:l + 1],
                                                              in1=self.rsd[:], op0=ALU.mult, op1=ALU.mult),
                 reads=[("ps", bo), "gn", "rsd"], writes=["ytmp"])
            s.op("dve", lambda: nc.vector.tensor_tensor(out=self.mixT[:, 4 + h, tsl], in0=self.ytmp[:], in1=self.sgate[:], op=ALU.mult),
                 reads=["ytmp", ("sg", 1)], writes=[("mixT", 4 + h, nb)])

    def mixer(self, l):
        nc, s = self.nc, self.s
        Win = self.w_in[l]
        s.dma("pool", self.poolw[:], self.pool_w[l].rearrange("g c d -> c g d"), writes=["poolw"], key="poolw")
        for gp in range(2):
            su = self.load_u(Win[:, gp * 256:(gp + 1) * 256], 256)
            for gi in range(2):
                self.pool_group(l, gp * 2 + gi, su)
        for hp in range(2):
            sF = self.load_u(Win[:, 1024 + hp * 256:1024 + (hp + 1) * 256], 256)
            sQ = self.load_u(Win[:, 512 + hp * 256:512 + (hp + 1) * 256], 256)
            sV = self.load_u(Win[:, 1536 + hp * 256:1536 + (hp + 1) * 256], 256)
            sG = self.load_u(Win[:, 2048 + hp * 256:2048 + (hp + 1) * 256], 256)
            for hi in range(2):
                self.hgrn_head(l, hp * 2 + hi, sF, sQ, sV, sG)
        dsl = [self.load_d(self.w_out[l][kc * 128:(kc + 1) * 128, :]) for kc in range(NKC)]
        for tt in range(NTT):
            for hh in range(2):
                b = self.psum()
                fns = [lambda kc=kc: nc.tensor.matmul(self.ps[b][:], lhsT=self.mixT[:, kc, tt * 128:(tt + 1) * 128],
                                                      rhs=self.dslot[dsl[kc]][:, hh * 512:(hh + 1) * 512],
                                                      start=(kc == 0), stop=(kc == NKC - 1)) for kc in range(NKC)]
                s.op("pe", fns, reads=[("mixT", kc, tt // 4) for kc in range(NKC)] + [("ds", dsl[kc]) for kc in range(NKC)],
                     writes=[("ps", b)])
                xs = self.X[:, tt, hh * 512:(hh + 1) * 512]
                s.op("dve", lambda: nc.vector.scalar_tensor_tensor(out=xs, in0=xs, scalar=ALPHA, in1=self.ps[b][:],
                                                                  op0=ALU.mult, op1=ALU.add),
                     reads=[("ps", b)], writes=[("X", tt, hh)])


def consts():
    ident = np.eye(128, dtype=np.float32)
    s_idx = np.arange(128)[:, None]
    t_idx = np.arange(128)[None, :]
    mask = ((s_idx // CH == t_idx // CH) & (s_idx <= t_idx)).astype(np.float32)
    rowm = np.zeros((128, 2), np.float32)
    rowm[:64, 0] = 1.0
    rowm[64:, 1] = 1.0
    return {"c_ident": ident, "c_mask": mask, "c_rowm": rowm}


def rc_table(first_steps):
    rc = np.zeros((128, 2, 64), np.float32)
    for st in range(2):
        for g, w in enumerate(POOL_W):
            cnt = np.minimum(np.arange(16) + 1, w) if st in first_steps else np.full(16, w)
            rc[:, st, g * 16:(g + 1) * 16] = 1.0 / cnt
    return rc


def lb_mask(layers):
    m = np.zeros((128, len(layers), DEPTH), np.float32)
    for j, l in enumerate(layers):
        m[:, j, 1:l + 1] = 1.0
    return m


PER_LAYER = ("w_in", "pool_w", "pool_scale", "hgrn_norm_g", "w_out", "ffn1_gate", "ffn1_up", "ffn1_down",
             "ffn2_gate", "ffn2_up", "ffn2_down", "ln_g", "ln_b")
_CACHE = {}


def kernel(**inputs):
    x = np.asarray(inputs["x"], dtype=np.float32)
    B, S, _ = x.shape
    nseg = S // TS
    NLOC = DEPTH // 2
    if "nc" not in _CACHE:
        _CACHE["nc"] = Prog(nseg=nseg, nlayers=NLOC, pipelined=True).build()
    nc = _CACHE["nc"]
    full = {k: np.asarray(v, dtype=np.float32) for k, v in inputs.items() if k != "x"}
    cst = consts()
    stage = []
    for r in range(2):
        layers = list(range(r * NLOC, (r + 1) * NLOC))
        m = {k: np.ascontiguousarray(full[k][layers]) for k in PER_LAYER}
        m["lb_param"] = np.ascontiguousarray(full["lb_param"])
        m.update(cst)
        m["c_lbmask"] = lb_mask(layers)
        m["c_rc"] = rc_table({r})
        sel = np.zeros((128, 2), np.float32)
        sel[:, r] = 1.0
        m["c_sel"] = sel
        stage.append(m)
    zeros = np.zeros((S, x.shape[2]), np.float32)
    in_maps = []
    for b in range(B):
        for r in range(2):
            m = dict(stage[r])
            m["x"] = np.ascontiguousarray(x[b]) if r == 0 else zeros
            in_maps.append(m)
    res = run_bass_kernel_spmd(nc, in_maps, core_ids=list(range(2 * B)))
    return np.stack([np.asarray(res.results[2 * b + 1]["out"], dtype=np.float32) for b in range(B)], axis=0)
```
